# Optimizing a Trainium2 kernel written in Bass

```python
import math
import jax
import jax.numpy as jnp
from jax import lax
import numpy as np

D_MODEL = 2048
BATCH = 4
SEQ = 2048
DEPTH = 2

GRID_W = 64
CTX_LEN = 256
Q_BLOCK = 128
EPS = 1e-6
ROPE_BASE = 10000.0
N_MOD = 6

DA_HEADS = 6
DA_HALF_DIM = 64
DA_V_DIM = 2 * DA_HALF_DIM
DA_WIDTH = DA_HEADS * DA_V_DIM

POOL_WINDOWS = (2, 4, 8, 16)
POOL_GROUP_DIM = 128
POOL_WIDTH = len(POOL_WINDOWS) * POOL_GROUP_DIM

GQA_Q_HEADS = 6
GQA_KV_HEADS = 2
GQA_GROUP = GQA_Q_HEADS // GQA_KV_HEADS
GQA_HEAD_DIM = 128
GQA_WIDTH = GQA_Q_HEADS * GQA_HEAD_DIM

MIX_WIDTH = DA_WIDTH + POOL_WIDTH + GQA_WIDTH

IN_SIZES = (DA_HEADS * 2 * DA_HALF_DIM, DA_HEADS * 2 * DA_HALF_DIM, DA_HEADS * DA_V_DIM,
            POOL_WIDTH, GQA_WIDTH, GQA_KV_HEADS * GQA_HEAD_DIM, GQA_KV_HEADS * GQA_HEAD_DIM)
IN_COLS = sum(IN_SIZES)
IN_SPLITS = tuple(sum(IN_SIZES[:i + 1]) for i in range(len(IN_SIZES) - 1))

D_FF = 5632
CONV_W = 3

kernel_name = "hybrid_diffattn_pool_gqa_convffn_prefix_dit"


def rms_norm(x, gain):
    xf = x.astype(jnp.float32)
    y = xf * lax.rsqrt(jnp.mean(xf * xf, axis=-1, keepdims=True) + EPS)
    return (y * gain.astype(jnp.float32)).astype(x.dtype)


def modulate(x, gain, shift, scale):
    return rms_norm(x, gain) * (1.0 + scale) + shift


def axial_angles(row, col, rot_dim):
    axis_dim = rot_dim // 2
    freqs = ROPE_BASE ** (-(jnp.arange(axis_dim // 2, dtype=jnp.float32) * 2.0 / axis_dim))
    return row[:, None] * freqs, col[:, None] * freqs


def rope_1d(x, ang):
    x1, x2 = jnp.split(x, 2, axis=-1)
    cos = jnp.cos(ang).astype(x.dtype)
    sin = jnp.sin(ang).astype(x.dtype)
    return jnp.concatenate([x1 * cos - x2 * sin, x2 * cos + x1 * sin], axis=-1)


def rope_2d(x, ang_row, ang_col):
    xr, xc = jnp.split(x, 2, axis=-1)
    return jnp.concatenate([rope_1d(xr, ang_row), rope_1d(xc, ang_col)], axis=-1)


def rope_da(t, ang):
    t1, t2 = jnp.split(t, 2, axis=-1)
    return jnp.concatenate([rope_2d(t1, *ang), rope_2d(t2, *ang)], axis=-1)


def block_attention(q, k, v):
    b, hk, g, sq, dk = q.shape
    nb = sq // Q_BLOCK
    scale = dk ** -0.5
    qb = q.reshape(b, hk, g, nb, Q_BLOCK, dk).transpose(3, 0, 1, 2, 4, 5)

    def one_block(qblk):
        s = jnp.einsum('bhgqd,bhkd->bhgqk', qblk, k).astype(jnp.float32) * scale
        p = jax.nn.softmax(s, axis=-1).astype(v.dtype)
        return jnp.einsum('bhgqk,bhkd->bhgqd', p, v)

    out = lax.map(one_block, qb)
    return out.transpose(1, 2, 3, 0, 4, 5).reshape(b, hk, g, sq, v.shape[-1])


def diff_attention(q, k, v, lam, subln, lambda_init):
    d = DA_HALF_DIM
    a1 = block_attention(q[:, :, None, :, :d], k[..., :d], v)
    a2 = block_attention(q[:, :, None, :, d:], k[..., d:], v)
    o = (a1 - lam.astype(a1.dtype) * a2)[:, :, 0]
    o = rms_norm(o, subln) * (1.0 - lambda_init)
    b, h, sq, dv = o.shape
    return o.transpose(0, 2, 1, 3).reshape(b, sq, h * dv)


def gqa_attention(q, k, v):
    b, _, sq, d = q.shape
    o = block_attention(q.reshape(b, GQA_KV_HEADS, GQA_GROUP, sq, d), k, v)
    return o.reshape(b, GQA_Q_HEADS, sq, d).transpose(0, 2, 1, 3).reshape(b, sq, GQA_WIDTH)


def pool_mixer(u, w_pool, pool_scale):
    b, t, _ = u.shape
    uf = u.astype(jnp.float32)
    cs = jnp.pad(jnp.cumsum(uf, axis=1), ((0, 0), (1, 0), (0, 0)))
    pos = jnp.arange(t)
    outs = []
    for g, w in enumerate(POOL_WINDOWS):
        lo = jnp.clip(pos - w // 2, 0, t)
        hi = jnp.clip(pos - w // 2 + w, 0, t)
        csg = cs[..., g * POOL_GROUP_DIM:(g + 1) * POOL_GROUP_DIM]
        mean = (csg[:, hi] - csg[:, lo]) / (hi - lo).astype(jnp.float32)[None, :, None]
        outs.append(mean - uf[..., g * POOL_GROUP_DIM:(g + 1) * POOL_GROUP_DIM])
    pooled = jnp.stack(outs, axis=2).astype(u.dtype)
    mixed = jnp.einsum('btgc,gcd->btgd', pooled, w_pool).reshape(b, t, POOL_WIDTH)
    return mixed * pool_scale


def token_mixer(h_ctx, h_lat, ang_da, ang_gqa, lp, lambda_init, with_ctx_out):
    b, n_ctx, _ = h_ctx.shape
    n_tok = n_ctx + h_lat.shape[1]
    proj = jnp.concatenate([h_ctx, h_lat], axis=1) @ lp['w_in']
    da_q, da_k, da_v, pool_in, g_q, g_k, g_v = jnp.split(proj, IN_SPLITS, axis=-1)

    def heads(t, n, d):
        return t.reshape(b, n_tok, n, d).transpose(0, 2, 1, 3)

    q = heads(da_q, DA_HEADS, 2 * DA_HALF_DIM)
    k = heads(da_k, DA_HEADS, 2 * DA_HALF_DIM)
    v = heads(da_v, DA_HEADS, DA_V_DIM)
    q_lat = rope_da(q[:, :, n_ctx:], ang_da)
    k_all = jnp.concatenate([k[:, :, :n_ctx], rope_da(k[:, :, n_ctx:], ang_da)], axis=2)
    f32 = jnp.float32
    lam = (jnp.exp(jnp.sum(lp['lq1'].astype(f32) * lp['lk1'].astype(f32)))
           - jnp.exp(jnp.sum(lp['lq2'].astype(f32) * lp['lk2'].astype(f32))) + lambda_init)

    gq = rms_norm(heads(g_q, GQA_Q_HEADS, GQA_HEAD_DIM), lp['q_norm'])
    gk = rms_norm(heads(g_k, GQA_KV_HEADS, GQA_HEAD_DIM), lp['k_norm'])
    gv = heads(g_v, GQA_KV_HEADS, GQA_HEAD_DIM)
    gq_lat = rope_2d(gq[:, :, n_ctx:], *ang_gqa)
    gk_all = jnp.concatenate([gk[:, :, :n_ctx], rope_2d(gk[:, :, n_ctx:], *ang_gqa)], axis=2)

    def merge(o_da, o_pool, o_gqa):
        return jnp.concatenate([o_da, o_pool, rms_norm(o_gqa, lp['gqa_out_norm'])], axis=-1) @ lp['w_out']

    y_lat = merge(diff_attention(q_lat, k_all, v, lam, lp['subln'], lambda_init),
                  pool_mixer(pool_in[:, n_ctx:], lp['w_pool'], lp['pool_scale']),
                  gqa_attention(gq_lat, gk_all, gv))
    if not with_ctx_out:
        return None, y_lat
    y_ctx = merge(diff_attention(q[:, :, :n_ctx], k[:, :, :n_ctx], v[:, :, :n_ctx], lam, lp['subln'], lambda_init),
                  pool_mixer(pool_in[:, :n_ctx], lp['w_pool'], lp['pool_scale']),
                  gqa_attention(gq[:, :, :n_ctx], gk[:, :, :n_ctx], gv[:, :, :n_ctx]))
    return y_ctx, y_lat


def conv_ffn(h, w_up, conv_w, conv_b, w_down):
    t = h.shape[1]
    u = h @ w_up
    pad = CONV_W // 2
    up = jnp.pad(u, ((0, 0), (pad, CONV_W - 1 - pad), (0, 0)))
    u = sum(up[:, j:j + t] * conv_w[j] for j in range(CONV_W)) + conv_b
    gate, val = jnp.split(u, 2, axis=-1)
    return (jax.nn.silu(gate) * val) @ w_down


def setup_inputs(seed: int = 0) -> dict:
    key = jax.random.key(seed)
    ks = jax.random.split(key, 25)
    f32 = jnp.float32

    def nrm(k, shape, scale):
        return jax.random.normal(k, shape, f32) * scale

    def gain(k, shape):
        return 1.0 + 0.05 * jax.random.normal(k, shape, f32)

    return {
        'x': nrm(ks[0], (BATCH, SEQ, D_MODEL), 1.0),
        'c': nrm(ks[1], (BATCH, D_MODEL), 1.0),
        'ctx': nrm(ks[2], (BATCH, CTX_LEN, D_MODEL), 1.0),
        'c_ctx': nrm(ks[3], (D_MODEL,), 1.0),
        'w_mod': nrm(ks[4], (DEPTH, D_MODEL, N_MOD * D_MODEL), 0.5 * D_MODEL ** -0.5),
        'b_mod': nrm(ks[5], (DEPTH, N_MOD * D_MODEL), 0.01),
        'norm_mix': gain(ks[6], (DEPTH, D_MODEL)),
        'norm_ffn': gain(ks[7], (DEPTH, D_MODEL)),
        'w_in': nrm(ks[8], (DEPTH, D_MODEL, IN_COLS), D_MODEL ** -0.5),
        'da_lambda_q1': nrm(ks[9], (DEPTH, DA_HALF_DIM), 0.1),
        'da_lambda_k1': nrm(ks[10], (DEPTH, DA_HALF_DIM), 0.1),
        'da_lambda_q2': nrm(ks[11], (DEPTH, DA_HALF_DIM), 0.1),
        'da_lambda_k2': nrm(ks[12], (DEPTH, DA_HALF_DIM), 0.1),
        'da_subln': gain(ks[13], (DEPTH, DA_V_DIM)),
        'gqa_q_norm': gain(ks[14], (DEPTH, GQA_HEAD_DIM)),
        'gqa_k_norm': gain(ks[15], (DEPTH, GQA_HEAD_DIM)),
        'pool_w': nrm(ks[16], (DEPTH, len(POOL_WINDOWS), POOL_GROUP_DIM, POOL_GROUP_DIM), POOL_GROUP_DIM ** -0.5),
        'pool_scale': gain(ks[17], (DEPTH, POOL_WIDTH)),
        'gqa_out_norm': gain(ks[18], (DEPTH, GQA_WIDTH)),
        'w_out': nrm(ks[19], (DEPTH, MIX_WIDTH, D_MODEL), MIX_WIDTH ** -0.5),
        'w_up': nrm(ks[20], (DEPTH, D_MODEL, 2 * D_FF), D_MODEL ** -0.5),
        'conv_w': nrm(ks[21], (DEPTH, CONV_W, 2 * D_FF), CONV_W ** -0.5),
        'conv_b': nrm(ks[22], (DEPTH, 2 * D_FF), 0.01),
        'w_down': nrm(ks[23], (DEPTH, D_FF, D_MODEL), D_FF ** -0.5),
        'final_norm': gain(ks[24], (D_MODEL,)),
    }


def reference(x, c, ctx, c_ctx, w_mod, b_mod, norm_mix, norm_ffn, w_in,
              da_lambda_q1, da_lambda_k1, da_lambda_q2, da_lambda_k2, da_subln,
              gqa_q_norm, gqa_k_norm, pool_w, pool_scale, gqa_out_norm, w_out,
              w_up, conv_w, conv_b, w_down, final_norm):
    n_lat = x.shape[1]
    ROWS = n_lat // GRID_W
    row = jnp.repeat(jnp.arange(ROWS, dtype=jnp.float32), GRID_W)
    col = jnp.tile(jnp.arange(GRID_W, dtype=jnp.float32), ROWS)
    ang_da = axial_angles(row, col, DA_HALF_DIM)
    ang_gqa = axial_angles(row, col, GQA_HEAD_DIM)
    silu_c = jax.nn.silu(c)
    silu_c_ctx = jax.nn.silu(c_ctx)

    for l in range(DEPTH):
        last = l == DEPTH - 1
        lambda_init = 0.8 - 0.6 * math.exp(-0.3 * l)
        mod_lat = (silu_c @ w_mod[l] + b_mod[l])[:, None, :]
        mod_ctx = (silu_c_ctx @ w_mod[l] + b_mod[l])[None, None, :]
        sh1, sc1, g1, sh2, sc2, g2 = jnp.split(mod_lat, N_MOD, axis=-1)
        csh1, csc1, cg1, csh2, csc2, cg2 = jnp.split(mod_ctx, N_MOD, axis=-1)
        lp = dict(w_in=w_in[l], w_out=w_out[l], lq1=da_lambda_q1[l], lk1=da_lambda_k1[l],
                  lq2=da_lambda_q2[l], lk2=da_lambda_k2[l], subln=da_subln[l],
                  q_norm=gqa_q_norm[l], k_norm=gqa_k_norm[l], w_pool=pool_w[l],
                  pool_scale=pool_scale[l], gqa_out_norm=gqa_out_norm[l])

        h_lat = modulate(x, norm_mix[l], sh1, sc1)
        h_ctx = modulate(ctx, norm_mix[l], csh1, csc1)
        y_ctx, y_lat = token_mixer(h_ctx, h_lat, ang_da, ang_gqa, lp, lambda_init, not last)
        x = x + g1 * y_lat
        x = x + g2 * conv_ffn(modulate(x, norm_ffn[l], sh2, sc2), w_up[l], conv_w[l], conv_b[l], w_down[l])
        if not last:
            ctx = ctx + cg1 * y_ctx
            ctx = ctx + cg2 * conv_ffn(modulate(ctx, norm_ffn[l], csh2, csc2), w_up[l], conv_w[l], conv_b[l], w_down[l])

    return rms_norm(x, final_norm)
```

```python
import math
from contextlib import ExitStack
import numpy as np
import ml_dtypes
import concourse.bass as bass
import concourse.mybir as mybir
from concourse.bass_utils import run_bass_kernel_spmd

F32 = mybir.dt.float32
BF16 = mybir.dt.bfloat16
ALU = mybir.AluOpType
AF = mybir.ActivationFunctionType
AX = mybir.AxisListType
NPBF = ml_dtypes.bfloat16

D = 2048
NCK = 16
DEPTH = 2
T = 1152
TE = 1156
DFF = 5632
NFC = 44
EPS = 1e-6
BLKS = [(0, 128), (128, 512), (640, 512)]
EBLKS = [(0, 130), (130, 342), (472, 342), (814, 342)]
ENGS = ['sync', 'scalar', 'vector', 'gpsimd', 'tensor']


class Sched:
    def __init__(self, nc, stack):
        self.nc = nc
        self.stack = stack
        self.q = {e: [] for e in ENGS}
        self.esem = {e: stack.enter_context(nc.semaphore('es_' + e)) for e in ENGS}
        self.dsem = {}
        self.lastw = {}
        self.readers = {}

    def _deps(self, eng, reads, writes, extra):
        deps = list(extra)
        for r in reads:
            t = self.lastw.get(r)
            if t is not None:
                deps.append(t)
        for w in writes:
            t = self.lastw.get(w)
            if t is not None:
                deps.append(t)
            deps.extend(self.readers.get(w, ()))
        out = []
        for t in deps:
            if t[0] == 'e':
                if t[1] == eng and eng in ('tensor', 'sync'):
                    continue
                self.q[t[1]][t[2]]['mark'] = True
            out.append(t)
        return out

    def _commit(self, tok, reads, writes):
        for r in reads:
            self.readers.setdefault(r, []).append(tok)
        for w in writes:
            self.lastw[w] = tok
            self.readers[w] = []

    def op(self, eng, fn, reads=(), writes=(), extra=()):
        deps = self._deps(eng, reads, writes, extra)
        idx = len(self.q[eng])
        self.q[eng].append(dict(fn=fn, deps=deps, mark=False, dma=None))
        tok = ('e', eng, idx)
        self._commit(tok, reads, writes)
        return tok

    def dma(self, eng, name, fn, reads=(), writes=(), extra=(), n=1):
        deps = self._deps(eng, reads, writes, extra)
        if name not in self.dsem:
            self.dsem[name] = [self.stack.enter_context(self.nc.semaphore('ds_' + name)), 0]
        self.dsem[name][1] += 16 * n
        tok = ('d', name, self.dsem[name][1])
        self.q[eng].append(dict(fn=fn, deps=deps, mark=False, dma=name))
        self._commit(tok, reads, writes)
        return tok

    def wait(self, eng, toks):
        return self.op(eng, None, extra=toks)

    def barrier(self):
        toks = []
        for e in ('scalar', 'vector', 'gpsimd', 'tensor'):
            for idx in range(len(self.q[e]) - 1, -1, -1):
                it = self.q[e][idx]
                if it['fn'] is not None and it['dma'] is None:
                    toks.append(('e', e, idx))
                    break
        for name, (sem, cnt) in self.dsem.items():
            toks.append(('d', name, cnt))
        for e in ENGS:
            self.q[e].append(dict(fn=None, deps=self._deps('none', (), (), toks), mark=False, dma=None))

    def emit(self):
        nc = self.nc
        val = {}
        for e in ENGS:
            c = 0
            for i, it in enumerate(self.q[e]):
                if it['mark']:
                    c += 1
                    val[(e, i)] = c
        q, esem, dsem = self.q, self.esem, self.dsem

        def replay(ename, eng):
            waited = {}
            for it in q[ename]:
                for t in it['deps']:
                    if t[0] == 'e':
                        key, v, sem = ('e', t[1]), val[(t[1], t[2])], esem[t[1]]
                    else:
                        key, v, sem = ('d', t[1]), t[2], dsem[t[1]][0]
                    if waited.get(key, 0) < v:
                        eng.wait_ge(sem, v)
                        waited[key] = v
                if it['fn'] is None:
                    continue
                r = it['fn'](eng)
                if it['dma'] is not None:
                    for ins in r:
                        ins.then_inc(dsem[it['dma']][0], 16)
                elif it['mark']:
                    r.then_inc(esem[ename], 1)

        with nc.Block() as block:
            @block.sync
            def _(e):
                replay('sync', e)

            @block.scalar
            def _(e):
                replay('scalar', e)

            @block.vector
            def _(e):
                replay('vector', e)

            @block.gpsimd
            def _(e):
                replay('gpsimd', e)

            @block.tensor
            def _(e):
                replay('tensor', e)


class Prog:
    def __init__(self):
        self.nc = bass.Bass("TRN2", target_bir_lowering=False)
        self.st = ExitStack()
        self.S = Sched(self.nc, self.st)
        self.n_uid = 0
        self.outs = []
        self.banks = [self.st.enter_context(self.nc.psum_tensor(f'bank{i}', [128, 512], F32)) for i in range(8)]

    def din(self, name, shape, dt):
        return self.nc.dram_tensor(name, list(shape), dt, kind="ExternalInput").ap()

    def dout(self, name, shape, dt):
        return self.nc.dram_tensor(name, list(shape), dt, kind="ExternalOutput").ap()

    def sb(self, name, shape, dt):
        if getattr(self, 'bump', None) is not None:
            v, nb = self._view(self.bump, shape, dt)
            self.bump += (nb + 63) // 64 * 64
            assert self.bump <= self.arena_bytes, (name, self.bump, self.arena_bytes)
            return v
        return self.st.enter_context(self.nc.sbuf_tensor(name, list(shape), dt))

    def arena_init(self, nbytes):
        self.arena_bytes = nbytes
        self.arena = self.st.enter_context(self.nc.sbuf_tensor('arena', [128, nbytes // 4], F32))
        self.bump = None

    def _view(self, off, shape, dt):
        n = 1
        for d in shape[1:]:
            n *= d
        esz = 4 if dt == F32 else 2
        nb = n * esz
        assert off % 4 == 0
        v = self.arena[:, off // 4:off // 4 + (nb + 3) // 4]
        if dt != F32:
            v = v.bitcast(dt)[:, 0:n]
        if len(shape) == 3:
            v = v.rearrange("p (a b) -> p a b", b=shape[2])
        return v, nb

    def phase(self, off, barrier=True):
        if barrier:
            self.S.barrier()
        self.bump = off

    def ps(self, name, shape, dt=F32):
        return self.st.enter_context(self.nc.psum_tensor(name, list(shape), dt))

    def bank(self, i, dt=F32):
        b = self.banks[i]
        if dt == BF16:
            return b[:, :].bitcast(BF16)[:, 0:512]
        return b

    def uid(self, p='u'):
        self.n_uid += 1
        return f'{p}{self.n_uid}'

    def finish(self):
        self.S.wait('sync', self.outs)
        self.S.emit()
        self.st.close()
        return self.nc


def act(P, out, in_, func, reads, writes, bias=None, scale=None, accum_out=None):
    kw = {}
    if bias is not None:
        kw['bias'] = bias
    if scale is not None:
        kw['scale'] = scale
    if accum_out is not None:
        kw['accum_out'] = accum_out
    return P.S.op('scalar', lambda e: e.activation(out=out, in_=in_, func=func, **kw), reads, writes)


def tt(P, eng, out, in0, in1, op, reads, writes):
    return P.S.op(eng, lambda e: e.tensor_tensor(out=out, in0=in0, in1=in1, op=op), reads, writes)


def stt(P, eng, out, in0, scalar, in1, op0, op1, reads, writes):
    return P.S.op(eng, lambda e: e.scalar_tensor_tensor(out=out, in0=in0, scalar=scalar, in1=in1, op0=op0, op1=op1), reads, writes)


def ts(P, eng, out, in0, s1, s2, op0, op1, reads, writes):
    if s2 is None:
        return P.S.op(eng, lambda e: e.tensor_scalar(out=out, in0=in0, scalar1=s1, scalar2=None, op0=op0), reads, writes)
    return P.S.op(eng, lambda e: e.tensor_scalar(out=out, in0=in0, scalar1=s1, scalar2=s2, op0=op0, op1=op1), reads, writes)


def cp(P, eng, out, in_, reads, writes):
    if eng == 'scalar':
        return P.S.op('scalar', lambda e: e.activation(out=out, in_=in_, func=AF.Copy), reads, writes)
    return P.S.op(eng, lambda e: e.tensor_copy(out=out, in_=in_), reads, writes)


def mm(P, out, lhsT, rhs, start, stop, reads, writes):
    return P.S.op('tensor', lambda e: e.matmul(out, lhsT=lhsT, rhs=rhs, start=start, stop=stop), reads, writes)


def load(P, eng, name, out, in_, writes, reads=()):
    return P.S.dma(eng, name, lambda e: [e.dma_start(out=out, in_=in_)], reads=reads, writes=writes)


def store(P, name, out, in_, reads):
    t = P.S.dma('sync', name, lambda e: [e.dma_start(out=out, in_=in_)], reads=reads)
    P.outs.append(t)
    return t


def consts(P):
    C = {}
    C['ones'] = P.sb('ones', [128, 128], BF16)
    C['ident'] = P.sb('ident', [128, 128], BF16)
    C['eps'] = P.sb('epsc', [128, 1], F32)
    ones, ident, eps = C['ones'], C['ident'], C['eps']
    P.S.op('vector', lambda e: e.memset(ones[:], 1.0), writes=['ones'])
    P.S.op('vector', lambda e: e.memset(eps[:], EPS), writes=['eps'])
    P.S.op('gpsimd', lambda e: e.memset(ident[:], 0.0), writes=['ident'])
    P.S.op('gpsimd', lambda e: e.affine_select(out=ident[:], in_=ident[:], pattern=[[-1, 128]],
                                               compare_op=ALU.not_equal, fill=1.0, base=0, channel_multiplier=1),
           reads=['ident'], writes=['ident'])
    return C


def emit_mod(P, C, ccT_d, wmod_d, bmod_d, modT_sb):
    S = P.S
    cc = P.sb('cc', [128, NCK, 5], F32)
    sc = P.sb('scc', [128, NCK, 5], BF16)
    bm = P.sb('bm', [128, 2, 12], F32)
    load(P, 'sync', 'cc', cc[:], ccT_d, ['cc'])
    load(P, 'sync', 'bm', bm[:], bmod_d, ['bm'])
    act(P, sc[:], cc[:], AF.Silu, ['cc'], ['scc'])
    wbuf = [P.sb(f'wm{i}', [128, NCK, 384], BF16) for i in range(2)]
    pm = [P.bank(i) for i in range(2)]
    n = 0
    for l in range(2):
        for qd in range(4):
            wb = wbuf[n % 2]
            src = wmod_d[l, :, qd * 384:(qd + 1) * 384].rearrange("(k p) c -> p k c", p=128)
            load(P, 'gpsimd', f'wm{n % 2}', wb[:], src, [('wm', n % 2)])
            for jj in range(3):
                j = qd * 3 + jj
                pp = pm[j % 2]
                for k in range(NCK):
                    mm(P, pp[:, 0:5], wb[:, k, jj * 128:(jj + 1) * 128], sc[:, k, :], k == 0, k == NCK - 1,
                       [('wm', n % 2), 'scc'], [('bank', j % 2)])
                o = (l * 12 + j) * 5
                act(P, modT_sb[:, o:o + 5], pp[:, 0:5], AF.Identity, [('bank', j % 2), 'bm'], ['modT'],
                    bias=bm[:, l, j:j + 1])
            n += 1


def mvcol(mv, l, m, ck, w):
    o = ((l * 6 + m) * NCK + ck) * 2 + w
    return mv[:, o:o + 1]


def emit_gs(P, mv, gains, gs):
    for l in range(DEPTH):
        for n in range(2):
            m = 1 if n == 0 else 4
            for w in range(2):
                o0 = ((l * 6 + m) * NCK) * 2
                src = mv[:, o0:o0 + 2 * NCK].rearrange("p (k w) -> p k w", w=2)[:, :, w]
                g0 = ((l * 2 + n) * NCK) * 2
                dst = gs[:, g0:g0 + 2 * NCK].rearrange("p (k w) -> p k w", w=2)[:, :, w]
                gsrc = gains[:, (l * 2 + n) * NCK:(l * 2 + n + 1) * NCK]
                stt(P, 'vector', dst, src, 1.0, gsrc, ALU.add, ALU.mult, ['mv', 'gains'], ['gs'])


def gscol(gs, l, n, ck, w):
    o = ((l * 2 + n) * NCK + ck) * 2 + w
    return gs[:, o:o + 1]


def emit_norm_mod(P, C, xT, hT, hoff, gcol, bcol, W, final_out=None):
    S = P.S
    sq, rstd, tmp, pss = W['sq'], W['rstd'], W['tmp'], W['ps_ssq']
    for bi, (c0, n) in enumerate(BLKS):
        for ck in range(NCK):
            s = ck % 2
            tt(P, 'gpsimd', sq[s][:, 0:n], xT[:, ck, c0:c0 + n], xT[:, ck, c0:c0 + n], ALU.mult,
               [('x', ck)], [('sq', s)])
            mm(P, pss[:, 0:n], C['ones'][:], sq[s][:, 0:n], ck == 0, ck == NCK - 1, [('sq', s), 'ones'], [('bank', 7)])
        act(P, rstd[:, c0:c0 + n], pss[:, 0:n], AF.Sqrt, [('bank', 7), 'eps'], [('rstd', bi)],
            bias=C['eps'][:, 0:1], scale=1.0 / D)
        P.S.op('vector', (lambda a: (lambda e: e.reciprocal(out=a, in_=a)))(rstd[:, c0:c0 + n]),
               reads=[('rstd', bi)], writes=[('rstd', bi)])
    for ck in range(NCK):
        for bi, (c0, n) in enumerate(BLKS):
            s = (ck * 3 + bi) % 2
            tt(P, 'vector', tmp[s][:, 0:n], xT[:, ck, c0:c0 + n], rstd[:, c0:c0 + n], ALU.mult, [('x', ck), ('rstd', bi)], [('tmp', s)])
            w = 1 if bi == 0 else 0
            if final_out is None:
                act(P, hT[:, ck, c0 + hoff[w]:c0 + n + hoff[w]], tmp[s][:, 0:n], AF.Identity, [('tmp', s), 'gs', 'mv'], [('h', ck)],
                    bias=bcol(ck, w), scale=gcol(ck, w))
            else:
                act(P, final_out[:, ck, c0:c0 + n], tmp[s][:, 0:n], AF.Copy, [('tmp', s), 'fng'], [('fo', ck)], scale=gcol(ck, 0))


def norm_work(P, scratch=None):
    W = {}
    W['sq'] = [P.sb(f'nsq{i}', [128, 512], BF16) for i in range(2)]
    if scratch is None:
        scratch = P.sb('nscr', [128, 2304], F32)
    W['rstd'] = scratch[:, 0:T]
    W['tmp'] = [scratch[:, T + i * 512:T + (i + 1) * 512] for i in range(2)]
    W['ps_ssq'] = P.bank(7)
    return W


PROJ_BLOCKS = [('daq', 0, 384, 0), ('daq', 384, 384, 384), ('dak', 768, 384, 0), ('dak', 1152, 384, 384),
               ('gq', 2816, 384, 0), ('gq', 3200, 384, 384), ('gk', 3584, 256, 0),
               ('v', 1536, 384, 0), ('v', 1920, 384, 384), ('v', 3840, 256, 768)]
PEXT_OFF = [(0, 128, 8), (128, 512, 152), (640, 512, 152 + 512)]


def emit_proj(P, C, l, hT, w_in_d, tabs, QT, KT_d, V_d, pext, scratch=None):
    S = P.S
    wb = [P.sb(f'wi{i}', [128, NCK, 384], BF16) for i in range(2)]
    kst = [P.sb(f'kst{i}', [128, 3, 128], BF16) for i in range(2)]
    vst = [P.sb(f'vst{i}', [128, 384], BF16) for i in range(2)]
    pb = [P.bank(i) for i in range(2)]
    pT = [P.bank(2 + i, BF16) for i in range(2)]
    if scratch is None:
        scratch = P.sb('pscr', [128, 2304], F32)
    raw = [scratch[:, i * 384:(i + 1) * 384] for i in range(2)]
    t1 = [scratch[:, (2 + i) * 384:(3 + i) * 384] for i in range(2)]
    t2 = [scratch[:, (4 + i) * 384:(5 + i) * 384] for i in range(2)]
    qk = [P.sb(f'qk{i}', [128, 384], BF16) for i in range(2)]
    st6 = [P.sb(f'st6{i}', [128, 4], F32) for i in range(2)]
    nb = 0
    it = 0
    for (kind, c0, n, dcol) in PROJ_BLOCKS:
        w = wb[nb % 2]
        wr = ('wi', nb % 2)
        src = w_in_d[l, :, c0:c0 + n].rearrange("(k p) c -> p k c", p=128)
        load(P, 'gpsimd', f'wi{nb % 2}', w[:, :, 0:n], src, [wr])
        nb += 1
        for t in range(9):
            s = it % 2
            it += 1
            p = pb[s]
            for k in range(NCK):
                mm(P, p[:, 0:n], hT[:, k, t * 128:(t + 1) * 128], w[:, k, 0:n], k == 0, k == NCK - 1,
                   [('h', k), wr], [('bank', s)])
            if kind == 'v':
                cp(P, 'scalar', vst[s][:, 0:n], p[:, 0:n], [('bank', s)], [('vst', s)])
                P.outs.append(P.S.dma('sync', f'vst{s}', (lambda o, i: (lambda e: [e.dma_start(out=o, in_=i)]))(
                    V_d[:, t, dcol:dcol + n], vst[s][:, 0:n]), reads=[('vst', s)]))
                continue
            r = raw[s]
            cp(P, 'scalar', r[:, 0:n], p[:, 0:n], [('bank', s)], [('raw', s)])
            q = qk[s]
            nh = n // 128
            src_t = r
            src_res = ('raw', s)
            if kind in ('gq', 'gk'):
                gname = 'qn' if kind == 'gq' else 'kn'
                tt(P, 'vector', t1[s][:, 0:n], r[:, 0:n], r[:, 0:n], ALU.mult, [('raw', s)], [('t1', s)])
                P.S.op('vector', (lambda o, i: (lambda e: e.tensor_reduce(out=o, in_=i, axis=AX.X, op=ALU.add)))(
                    st6[s][:, 0:nh], t1[s][:, 0:n].rearrange("p (h d) -> p h d", d=128)),
                    reads=[('t1', s)], writes=[('st6', s)])
                act(P, st6[s][:, 0:nh], st6[s][:, 0:nh], AF.Sqrt, [('st6', s), 'eps'], [('st6', s)],
                    bias=C['eps'][:, 0:1], scale=1.0 / 128)
                P.S.op('vector', (lambda a: (lambda e: e.reciprocal(out=a, in_=a)))(st6[s][:, 0:nh]),
                       reads=[('st6', s)], writes=[('st6', s)])
                r3 = r[:, 0:n].rearrange("p (h d) -> p h d", d=128)
                t13 = t1[s][:, 0:n].rearrange("p (h d) -> p h d", d=128)
                tt(P, 'vector', t13, r3, st6[s][:, 0:nh].unsqueeze(2).broadcast_to([128, nh, 128]), ALU.mult,
                   [('raw', s), ('st6', s)], [('t1', s)])
                gt = tabs[gname]
                tt(P, 'vector', t13, t13, gt[:, :].unsqueeze(1).broadcast_to([128, nh, 128]), ALU.mult,
                   [('t1', s), 'tabs'], [('t1', s)])
                src_t = t1[s]
                src_res = ('t1', s)
            if t == 0:
                cp(P, 'vector', q[:, 0:n], src_t[:, 0:n], [src_res], [('qk', s)])
            else:
                if kind in ('daq', 'dak'):
                    i_ = 16
                    ct, sg = tabs['cda'], tabs['sda']
                    rep = 64
                else:
                    i_ = 32
                    ct, sg = tabs['cg'], tabs['sg']
                    rep = 128
                a = n // rep
                xv = src_t[:, 0:n].rearrange("p (a r j i) -> p a r j i", r=2, j=2, i=i_)
                ov = t2[s][:, 0:n].rearrange("p (a r j i) -> p a r j i", r=2, j=2, i=i_)
                sgv = sg[:, t - 1, :].rearrange("p (r j i) -> p r j i", r=2, j=2)
                ctv = ct[:, t - 1, :].rearrange("p (r i) -> p r i", r=2).unsqueeze(1).broadcast_to([128, a, 2, i_])
                tt(P, 'vector', ov[:, :, :, 0, :], xv[:, :, :, 1, :],
                   sgv[:, :, 0, :].unsqueeze(1).broadcast_to([128, a, 2, i_]), ALU.mult, [src_res, 'tabs'], [('t2', s)])
                tt(P, 'vector', ov[:, :, :, 1, :], xv[:, :, :, 0, :],
                   sgv[:, :, 1, :].unsqueeze(1).broadcast_to([128, a, 2, i_]), ALU.mult, [src_res, 'tabs'], [('t2', s)])
                dst_t = raw[s] if src_res != ('raw', s) else t1[s]
                dst_res = ('raw', s) if src_res != ('raw', s) else ('t1', s)
                dv = dst_t[:, 0:n].rearrange("p (a r j i) -> p a r j i", r=2, j=2, i=i_)
                for j in range(2):
                    tt(P, 'vector', dv[:, :, :, j, :], xv[:, :, :, j, :], ctv, ALU.mult, [src_res, 'tabs'], [dst_res])
                tt(P, 'vector', q[:, 0:n], dst_t[:, 0:n], t2[s][:, 0:n], ALU.add, [dst_res, ('t2', s)], [('qk', s)])
            pt = pT[s]
            for hh in range(nh):
                P.S.op('tensor', (lambda o, i: (lambda e: e.transpose(o, i, C['ident'][:])))(
                    pt[:, hh * 128:(hh + 1) * 128], q[:, hh * 128:(hh + 1) * 128]),
                    reads=[('qk', s), 'ident'], writes=[('bank', 2 + s)])
            h0 = dcol // 128
            src3 = pt[:, 0:n].rearrange("p (h c) -> p h c", c=128)
            if kind in ('daq', 'gq'):
                hq = h0 if kind == 'daq' else 6 + h0
                cp(P, 'scalar', QT[:, hq:hq + nh, t * 128:(t + 1) * 128], src3, [('bank', 2 + s)], ['QT'])
            else:
                hk_ = h0 if kind == 'dak' else 6 + h0
                cp(P, 'scalar', kst[s][:, 0:nh, :], src3, [('bank', 2 + s)], [('kst', s)])
                P.outs.append(P.S.dma('sync', f'kst{s}', (lambda o, i: (lambda e: [e.dma_start(out=o, in_=i)]))(
                    KT_d[:, hk_:hk_ + nh, t * 128:(t + 1) * 128], kst[s][:, 0:nh, :]), reads=[('kst', s)]))
    for g in range(4):
        w = wb[nb % 2]
        wr = ('wi', nb % 2)
        c0 = 2304 + g * 128
        src = w_in_d[l, :, c0:c0 + 128].rearrange("(k p) c -> p k c", p=128)
        load(P, 'gpsimd', f'wi{nb % 2}', w[:, :, 0:128], src, [wr])
        nb += 1
        for (b0, n, e0) in PEXT_OFF:
            s = it % 2
            it += 1
            p = pb[s]
            for k in range(NCK):
                mm(P, p[:, 0:n], w[:, k, 0:128], hT[:, k, b0:b0 + n], k == 0, k == NCK - 1, [('h', k), wr], [('bank', s)])
            cp(P, 'scalar', pext[:, g, e0:e0 + n], p[:, 0:n], [('bank', s)], ['pext'])


def emit_attn(P, C, l, QT, KTf_d, Vf_d, prm, mixT):
    S = P.S
    kt1 = P.sb('kt', [128, 2304], BF16)
    vt1 = P.sb('vt', [128, 18, 128], BF16)
    kt = [kt1, kt1]
    vt = [vt1, vt1]
    pS = [P.bank(i) for i in range(2)]
    pO = [P.bank(2 + i) for i in range(2)]
    pD = [P.bank(4 + i) for i in range(2)]
    pN = P.bank(6)
    pt = [P.sb(f'pt{i}', [128, 512], BF16) for i in range(3)]
    rden1 = P.sb('rden', [128, 512], F32)
    rden = [rden1, rden1]
    osq = [P.sb(f'osq{i}', [128, 512], BF16) for i in range(2)]
    go = P.sb('go', [128, 6, T], F32)
    aa = [go[:, 3, :], go[:, 4, :]]
    od = go[:, 5, :]
    rn = rden1
    cnt = {'u': 0, 'p': 0}

    def unit(kts, vts, hk, rows, qhead, scale, dst, dst_res):
        for (q0, qn) in BLKS:
            kcs = [0, 9] if q0 == 0 else list(range(18))
            u = cnt['u'] % 2
            cnt['u'] += 1
            pend = None
            for i, kc in enumerate(kcs):
                sl = cnt['p'] % 2
                psl = cnt['p'] % 3
                cnt['p'] += 1
                mm(P, pS[sl][:, 0:qn], kts[rows, kc * 128:(kc + 1) * 128], QT[rows, qhead, q0:q0 + qn], True, True,
                   ['kt', 'QT'], [('bank', sl)])
                act(P, pt[psl][:, 0:qn], pS[sl][:, 0:qn], AF.Exp, [('bank', sl)], [('pt', psl)], scale=scale)
                if pend is not None:
                    pkc, ppsl, first = pend
                    mm(P, pO[u][:, 0:qn], vts[:, pkc, :], pt[ppsl][:, 0:qn], first, False, ['vt', ('pt', ppsl)], [('bank', 2 + u)])
                    mm(P, pD[u][:, 0:qn], C['ones'][:], pt[ppsl][:, 0:qn], first, False, ['ones', ('pt', ppsl)], [('bank', 4 + u)])
                pend = (kc, psl, i == 0)
            pkc, ppsl, first = pend
            mm(P, pO[u][:, 0:qn], vts[:, pkc, :], pt[ppsl][:, 0:qn], first, True, ['vt', ('pt', ppsl)], [('bank', 2 + u)])
            mm(P, pD[u][:, 0:qn], C['ones'][:], pt[ppsl][:, 0:qn], first, True, ['ones', ('pt', ppsl)], [('bank', 4 + u)])
            P.S.op('vector', (lambda o, i: (lambda e: e.reciprocal(out=o, in_=i)))(rden[u][:, 0:qn], pD[u][:, 0:qn]),
                   reads=[('bank', 4 + u)], writes=['rden'])
            tt(P, 'vector', dst[:, q0:q0 + qn], pO[u][:, 0:qn], rden[u][:, 0:qn], ALU.mult,
               [('bank', 2 + u), 'rden'], [dst_res])

    for hk in range(8):
        kts, vts = kt[hk % 2], vt[hk % 2]
        load(P, 'sync', 'kt', kts[:, :], KTf_d[:, hk, :], ['kt'])
        load(P, 'sync', 'vt', vts[:, :, :], Vf_d[:, :, hk * 128:(hk + 1) * 128], ['vt'])
        if hk < 6:
            for half in range(2):
                rows = slice(half * 64, half * 64 + 64)
                unit(kts, vts, hk, rows, hk, 0.125, aa[half], ('go', 3 + half))
            stt(P, 'vector', od[:, :], aa[1][:, :], prm['neglam'][:, 0:1], aa[0][:, :], ALU.mult, ALU.add,
                [('go', 3), ('go', 4), 'prm'], [('go', 5)])
            for bi, (q0, qn) in enumerate(BLKS):
                s = bi % 2
                tt(P, 'gpsimd', osq[s][:, 0:qn], od[:, q0:q0 + qn], od[:, q0:q0 + qn], ALU.mult, [('go', 5)], [('osq', s)])
                mm(P, pN[:, 0:qn], C['ones'][:], osq[s][:, 0:qn], True, True, ['ones', ('osq', s)], [('bank', 6)])
                act(P, rn[:, 0:qn], pN[:, 0:qn], AF.Sqrt, [('bank', 6), 'eps'], ['rden'], bias=C['eps'][:, 0:1], scale=1.0 / 128)
                P.S.op('vector', (lambda a: (lambda e: e.reciprocal(out=a, in_=a)))(rn[:, 0:qn]), reads=['rden'], writes=['rden'])
                stt(P, 'vector', mixT[:, hk, q0:q0 + qn], od[:, q0:q0 + qn], prm['subl'][:, 0:1], rn[:, 0:qn],
                    ALU.mult, ALU.mult, [('go', 5), 'rden', 'prm'], [('mix', hk)])
        else:
            g = hk - 6
            for j in range(3):
                qh = 3 * g + j
                unit(kts, vts, hk, slice(0, 128), 6 + qh, 128.0 ** -0.5, go[:, qh, :], ('go', qh))
    for bi, (q0, qn) in enumerate(BLKS):
        for qh in range(6):
            s = qh % 2
            tt(P, 'gpsimd', osq[s][:, 0:qn], go[:, qh, q0:q0 + qn], go[:, qh, q0:q0 + qn], ALU.mult, [('go', qh)], [('osq', s)])
            mm(P, pN[:, 0:qn], C['ones'][:], osq[s][:, 0:qn], qh == 0, qh == 5, ['ones', ('osq', s)], [('bank', 6)])
        act(P, rn[:, 0:qn], pN[:, 0:qn], AF.Sqrt, [('bank', 6), 'eps'], ['rden'], bias=C['eps'][:, 0:1], scale=1.0 / 768)
        P.S.op('vector', (lambda a: (lambda e: e.reciprocal(out=a, in_=a)))(rn[:, 0:qn]), reads=['rden'], writes=['rden'])
        for qh in range(6):
            stt(P, 'vector', mixT[:, 10 + qh, q0:q0 + qn], go[:, qh, q0:q0 + qn], prm['gon'][:, qh:qh + 1], rn[:, 0:qn],
                ALU.mult, ALU.mult, [('go', qh), 'rden', 'prm'], [('mix', 10 + qh)])


PE_CTX = 144
PE_W = 144 + 1040


def emit_pool(P, C, l, pext, invc, wpool, pscale, mixT):
    sa = P.sb('pla', [128, PE_W], F32)
    sbb = P.sb('plb', [128, PE_W], F32)
    pl = P.sb('plp', [128, T], F32)
    plb = P.sb('plpb', [128, T], BF16)
    pp = [P.bank(i) for i in range(2)]
    for g, w in enumerate((2, 4, 8, 16)):
        u = pext[:, g, :]
        cur, cur_res = u, 'pext'
        step = 1
        bufs = [(sa, 'pla'), (sbb, 'plb')]
        bi = 0
        L = PE_W
        while step < w:
            dst, dres = bufs[bi % 2]
            bi += 1
            n = L - step
            tt(P, 'vector', dst[:, 0:n], cur[:, 0:n], cur[:, step:step + n], ALU.add, [cur_res], [dres])
            cur, cur_res = dst, dres
            L = n
            step *= 2
        for (o0, e0, n) in ((0, 8, 128), (128, PE_CTX + 8, 1024)):
            st_ = e0 - w // 2
            tt(P, 'vector', pl[:, o0:o0 + n], cur[:, st_:st_ + n], invc[:, g, o0:o0 + n], ALU.mult, [cur_res, 'invc'], ['plp'])
            tt(P, 'vector', plb[:, o0:o0 + n], pl[:, o0:o0 + n], u[:, e0:e0 + n], ALU.subtract, ['plp', 'pext'], ['plpb'])
        for bi2, (q0, qn) in enumerate(BLKS):
            s = bi2 % 2
            mm(P, pp[s][:, 0:qn], wpool[:, g, :], plb[:, q0:q0 + qn], True, True, ['plpb', 'wpool'], [('bank', s)])
            act(P, mixT[:, 6 + g, q0:q0 + qn], pp[s][:, 0:qn], AF.Copy, [('bank', s), 'pscale'], [('mix', 6 + g)],
                scale=pscale[:, g:g + 1])


def emit_outproj(P, C, l, mixT, w_out_d, gcol, xT):
    wb = [P.sb(f'wo{i}', [128, NCK, 256], BF16) for i in range(2)]
    po = [P.bank(2 + i) for i in range(2)]
    it = 0
    for ob in range(8):
        w = wb[ob % 2]
        wr = ('wo', ob % 2)
        src = w_out_d[l, :, ob * 256:(ob + 1) * 256].rearrange("(k p) c -> p k c", p=128)
        load(P, 'gpsimd', f'wo{ob % 2}', w[:, :, :], src, [wr])
        for oo in range(2):
            oc = ob * 2 + oo
            for bi, (q0, qn) in enumerate(BLKS):
                s = it % 2
                it += 1
                for k in range(NCK):
                    mm(P, po[s][:, 0:qn], w[:, k, oo * 128:(oo + 1) * 128], mixT[:, k, q0:q0 + qn], k == 0, k == NCK - 1,
                       [('mix', k), wr], [('bank', 2 + s)])
                wsel = 1 if bi == 0 else 0
                stt(P, 'vector', xT[:, oc, q0:q0 + qn], po[s][:, 0:qn], gcol(oc, wsel), xT[:, oc, q0:q0 + qn],
                    ALU.mult, ALU.add, [('bank', 2 + s), ('x', oc), 'mv'], [('x', oc)])


def emit_ffn(P, C, l, h2e, w_up_d, w_down_d, cw, gcol, xT, wgv=None):
    GC = 2
    if wgv is None:
        wg = [P.sb(f'wg{i}', [128, NCK, GC * 128], BF16) for i in range(2)]
        wv = [P.sb(f'wv{i}', [128, NCK, GC * 128], BF16) for i in range(2)]
    else:
        wg, wv = wgv
    wd = [P.sb(f'wd{i}', [128, GC, D], BF16) for i in range(2)]
    aT = [P.sb(f'aT{i}', [128, GC, TE], BF16) for i in range(2)]
    rawg1 = P.sb('rawg', [128, TE], F32)
    rawv1 = P.sb('rawv', [128, TE], F32)
    rawg = [rawg1, rawg1]
    rawv = [rawv1, rawv1]
    accg = [P.sb(f'accg{i}', [128, TE], F32) for i in range(2)]
    accv = [P.sb(f'accv{i}', [128, TE], F32) for i in range(2)]
    pu = [P.bank(i) for i in range(4)]
    pdn = [P.bank(4 + i) for i in range(2)]
    nu = 0
    nd = 0
    DB = [(1, 128, 0, 1), (131, 512, 128, 0), (643, 512, 640, 0)]
    for grp in range(NFC // GC):
        gs_ = grp % 2
        f0 = grp * GC * 128
        fw = GC * 128
        load(P, 'gpsimd', f'wg{gs_}', wg[gs_][:, :, :], w_up_d[l, :, f0:f0 + fw].rearrange("(k p) c -> p k c", p=128), [('wg', gs_)])
        load(P, 'gpsimd', f'wv{gs_}', wv[gs_][:, :, :], w_up_d[l, :, DFF + f0:DFF + f0 + fw].rearrange("(k p) c -> p k c", p=128), [('wv', gs_)])
        load(P, 'gpsimd', f'wd{gs_}', wd[gs_][:, :, :], w_down_d[l, f0:f0 + fw, :].rearrange("(f p) c -> p f c", p=128), [('wd', gs_)])
        for fcg in range(GC):
            fc = grp * GC + fcg
            s = fc % 2
            for (wt, wres, rw, rres, ac, ares, cidx) in ((wg[gs_], ('wg', gs_), rawg[s], 'rawg', accg[s], ('accg', s), fc),
                                                          (wv[gs_], ('wv', gs_), rawv[s], 'rawv', accv[s], ('accv', s), NFC + fc)):
                for (e0, en) in EBLKS:
                    ps_ = nu % 4
                    nu += 1
                    for k in range(NCK):
                        mm(P, pu[ps_][:, 0:en], wt[:, k, fcg * 128:(fcg + 1) * 128], h2e[:, k, e0:e0 + en], k == 0, k == NCK - 1,
                           [('h2', k), wres], [('bank', ps_)])
                    cp(P, 'scalar', rw[:, e0:e0 + en], pu[ps_][:, 0:en], [('bank', ps_)], [rres])
                    act(P, ac[:, e0:e0 + en], pu[ps_][:, 0:en], AF.Identity, [('bank', ps_), 'cw'], [ares],
                        bias=cw[:, cidx, 3:4], scale=cw[:, cidx, 1:2])
                eng = 'vector'
                stt(P, eng, ac[:, 1:TE - 1], rw[:, 0:TE - 2], cw[:, cidx, 0:1], ac[:, 1:TE - 1], ALU.mult, ALU.add,
                    [rres, ares, 'cw'], [ares])
                stt(P, eng, ac[:, 1:TE - 1], rw[:, 2:TE], cw[:, cidx, 2:3], ac[:, 1:TE - 1], ALU.mult, ALU.add,
                    [rres, ares, 'cw'], [ares])
            act(P, accg[s][:, 1:TE - 1], accg[s][:, 1:TE - 1], AF.Silu, [('accg', s)], [('accg', s)])
            tt(P, 'vector', aT[gs_][:, fcg, 1:TE - 1], accv[s][:, 1:TE - 1], accg[s][:, 1:TE - 1], ALU.mult,
               [('accv', s), ('accg', s)], [('aT', gs_)])
        for oc in range(NCK):
            for (e0, n, q0, wsel) in DB:
                s = nd % 2
                nd += 1
                for fcg in range(GC):
                    mm(P, pdn[s][:, 0:n], wd[gs_][:, fcg, oc * 128:(oc + 1) * 128], aT[gs_][:, fcg, e0:e0 + n], fcg == 0, fcg == GC - 1,
                       [('aT', gs_), ('wd', gs_)], [('bank', 4 + s)])
                stt(P, 'vector', xT[:, oc, q0:q0 + n], pdn[s][:, 0:n], gcol(oc, wsel), xT[:, oc, q0:q0 + n], ALU.mult, ALU.add,
                    [('bank', 4 + s), ('x', oc), 'mv'], [('x', oc)])


MV_N = DEPTH * 6 * NCK * 2
GS_N = DEPTH * 2 * NCK * 2


def load_small(P, name, shape, dt=F32, eng='sync', src_dt=None):
    d = P.din(name, shape, src_dt or dt)
    t = P.sb(name + '_s', shape, dt)
    load(P, eng, name, t[:], d, [name])
    return t


def build_mod():
    P = Prog()
    C = consts(P)
    ccT = P.din('ccT', [128, NCK, 5], F32)
    wmod = P.din('wmod', [2, D, 1536], F32)
    bmod = P.din('bmod', [128, 2, 12], F32)
    modT_d = P.dout('modT', [128, 120], F32)
    modT = P.sb('modT_s', [128, 120], F32)
    emit_mod(P, C, ccT, wmod, bmod, modT)
    store(P, 'o_mod', modT_d, modT[:], ['modT'])
    return P.finish()


def load_x(P, xT_d, xT=None):
    if xT is None:
        xT = P.sb('xT_s', [128, NCK, T], F32)
    for ck in range(NCK):
        load(P, 'sync', f'x{ck % 4}', xT[:, ck, :], xT_d[:, ck, :], [('x', ck)])
    return xT


ARENA = 131600
OFF_QT = 36864
OFF_PEXT = 64512
OFF_LOC = 83456
OFF_H2 = 36864
OFF_FFN = 73856


def load_tabs(P):
    tabs = {}
    for nm, shp in (('cda', [128, 8, 32]), ('sda', [128, 8, 64]), ('cg', [128, 8, 64]), ('sg', [128, 8, 128]),
                    ('qn', [128, 128]), ('kn', [128, 128])):
        d = P.din(nm, shp, F32)
        t = P.sb(nm + '_s', shp, F32)
        load(P, 'sync', 'tabs', t[:] if len(shp) == 2 else t[:, :, :], d, ['tabs'])
        tabs[nm] = t
    return tabs


def build_proj(l):
    P = Prog()
    C = consts(P)
    xT_d = P.din('xT', [128, NCK, T], F32)
    w_in = P.din('w_in', [1, D, 4096], F32)
    QT_d = P.dout('QT', [128, 12, T], BF16)
    KT_d = P.dout('KT', [128, 8, T], BF16)
    V_d = P.dout('V', [128, 9, 1024], BF16)
    pext_d = P.dout('pexto', [128, 4, PE_W], F32)
    mv = load_small(P, 'mv', [128, MV_N])
    gains = load_small(P, 'gains', [128, DEPTH * 2 * NCK])
    gs = P.sb('gs', [128, GS_N], F32)
    emit_gs(P, mv, gains, gs)
    xT = load_x(P, xT_d)
    P.arena_init(133000)
    P.phase(0)
    hT = P.sb('hT', [128, NCK, T], BF16)
    QT = P.sb('QT_s', [128, 12, T], BF16)
    pext = P.sb('pext_s', [128, 4, PE_W], F32)
    assert P.bump == OFF_LOC, P.bump
    P.S.op('gpsimd', lambda e: e.memset(pext[:, :, :], 0.0), writes=['pext'])
    scr = P.sb('scr', [128, 2304], F32)
    tabs = load_tabs(P)
    mark = P.bump
    W = norm_work(P, scr)
    emit_norm_mod(P, C, xT, hT, (0, 0), lambda ck, w: gscol(gs, l, 0, ck, w), lambda ck, w: mvcol(mv, l, 0, ck, w), W)
    P.S.barrier()
    P.bump = mark
    emit_proj(P, C, 0, hT, w_in, tabs, QT, KT_d, V_d, pext, scr)
    store(P, 'o_q', QT_d, QT[:, :, :], ['QT'])
    store(P, 'o_p', pext_d, pext[:, :, :], ['pext'])
    return P.finish()


def emit_prm(P, l, lam4, subln):
    li = 0.8 - 0.6 * math.exp(-0.3 * l)
    lt = P.sb(f'lt{l}', [128, 2, 64], F32)
    ls = P.sb(f'ls{l}', [128, 2], F32)
    neglam = P.sb(f'neglam{l}', [128, 1], F32)
    subl = P.sb(f'subl{l}', [128, 1], F32)
    tt(P, 'vector', lt[:, 0, :], lam4[:, 0, :], lam4[:, 1, :], ALU.mult, ['lam4'], ['lt'])
    tt(P, 'vector', lt[:, 1, :], lam4[:, 2, :], lam4[:, 3, :], ALU.mult, ['lam4'], ['lt'])
    P.S.op('vector', lambda e: e.tensor_reduce(out=ls[:, :], in_=lt[:, :, :], axis=AX.X, op=ALU.add), reads=['lt'], writes=['ls'])
    act(P, ls[:, :], ls[:, :], AF.Exp, ['ls'], ['ls'])
    tt(P, 'vector', neglam[:, :], ls[:, 1:2], ls[:, 0:1], ALU.subtract, ['ls'], ['neglam'])
    ts(P, 'vector', neglam[:, :], neglam[:, :], -li, None, ALU.add, None, ['neglam'], ['neglam'])
    ts(P, 'vector', subl[:, :], subln[:, :], 1.0 - li, None, ALU.mult, None, ['subln'], ['subl'])
    return neglam, subl


def build_attn(l):
    P = Prog()
    C = consts(P)
    xT_d = P.din('xT', [128, NCK, T], F32)
    QT_d = P.din('QT', [128, 12, T], BF16)
    KTf = P.din('KTf', [128, 8, 2 * T], BF16)
    Vf = P.din('Vf', [128, 18, 1024], BF16)
    w_out = P.din('w_out', [1, D, D], F32)
    pext_d = P.din('pext', [128, 4, PE_W], F32)
    invc_d = P.din('invc', [128, 4, T], F32)
    wpool_d = P.din('wpool', [4, 128, 128], F32)
    xo_d = P.dout('xTo', [128, NCK, T], F32)
    h2_d = P.dout('h2e', [128, NCK, TE], BF16)
    mv = load_small(P, 'mv', [128, MV_N])
    gains = load_small(P, 'gains', [128, DEPTH * 2 * NCK])
    pscale = load_small(P, 'pscale', [128, 4])
    lam4 = load_small(P, 'lam4', [128, 4, 64])
    subln = load_small(P, 'subln', [128, 1])
    gon = load_small(P, 'gon', [128, 6])
    gs = P.sb('gs', [128, GS_N], F32)
    emit_gs(P, mv, gains, gs)
    neglam, subl = emit_prm(P, l, lam4, subln)
    P.S.op('vector', lambda e: e.memset(C['eps'][:], EPS), reads=['neglam', 'subl', 'gon'], writes=['prm', 'eps'])
    prm = dict(neglam=neglam, subl=subl, gon=gon)
    xT = load_x(P, xT_d)
    P.arena_init(ARENA)
    P.phase(0)
    mixT = P.sb('mixT', [128, NCK, T], BF16)
    QT = P.sb('QT_s', [128, 12, T], BF16)
    pext = P.sb('pext_s', [128, 4, PE_W], F32)
    load(P, 'sync', 'qt', QT[:, :, :], QT_d, ['QT'])
    load(P, 'sync', 'pext', pext[:, :, :], pext_d, ['pext'])
    emit_attn(P, C, l, QT, KTf, Vf, prm, mixT)
    P.phase(OFF_LOC)
    invc = P.sb('invc_s', [128, 4, T], F32)
    wpool = P.sb('wpool_s', [128, 4, 128], BF16)
    load(P, 'sync', 'invc', invc[:, :, :], invc_d, ['invc'])
    load(P, 'gpsimd', 'wpool', wpool[:, :, :], wpool_d.rearrange("g c d -> c g d"), ['wpool'])
    emit_pool(P, C, l, pext, invc, wpool, pscale, mixT)
    P.phase(OFF_LOC)
    emit_outproj(P, C, 0, mixT, w_out, lambda oc, w: mvcol(mv, l, 2, oc, w), xT)
    P.phase(OFF_H2)
    h2e = P.sb('h2e_s', [128, NCK, TE], BF16)
    P.S.op('gpsimd', lambda e: e.memset(h2e[:, :, :], 0.0), writes=[('h', ck) for ck in range(NCK)])
    P.phase(OFF_LOC, barrier=False)
    W = norm_work(P)
    emit_norm_mod(P, C, xT, h2e, (3, 1), lambda ck, w: gscol(gs, l, 1, ck, w), lambda ck, w: mvcol(mv, l, 3, ck, w), W)
    store(P, 'o_x', xo_d, xT[:], [('x', ck) for ck in range(NCK)])
    store(P, 'o_h2', h2_d, h2e[:, :, :], [('h', ck) for ck in range(NCK)])
    return P.finish()


def build_ffn(l, final):
    P = Prog()
    C = consts(P)
    xT_d = P.din('xT', [128, NCK, T], F32)
    h2_d = P.din('h2e', [128, NCK, TE], BF16)
    w_up = P.din('w_up', [1, D, 2 * DFF], F32)
    w_down = P.din('w_down', [1, DFF, D], F32)
    xo_d = P.dout('xTo', [128, NCK, T], F32)
    mv = load_small(P, 'mv', [128, MV_N])
    cw = load_small(P, 'cw', [128, 2 * NFC, 4])
    fng = load_small(P, 'fng', [128, NCK])
    xT = load_x(P, xT_d)
    P.arena_init(ARENA)
    P.phase(0)
    wg = [P.sb(f'wg{i}', [128, NCK, 256], BF16) for i in range(2)]
    wv = [P.sb(f'wv{i}', [128, NCK, 256], BF16) for i in range(2)]
    P.phase(OFF_H2, barrier=False)
    h2e = P.sb('h2e_s', [128, NCK, TE], BF16)
    assert P.bump <= OFF_FFN, P.bump
    for ck in range(NCK):
        load(P, 'sync', f'h2{ck % 4}', h2e[:, ck, :], h2_d[:, ck, :], [('h2', ck)])
    P.phase(OFF_FFN, barrier=False)
    emit_ffn(P, C, 0, h2e, w_up, w_down, cw, lambda oc, w: mvcol(mv, l, 5, oc, w), xT, wgv=(wg, wv))
    if final:
        P.phase(0)
        fo = P.sb('fo', [128, NCK, T], F32)
        W = norm_work(P)
        emit_norm_mod(P, C, xT, None, (0, 0), lambda ck, w: fng[:, ck:ck + 1], None, W, final_out=fo)
        store(P, 'o_x', xo_d, fo[:, :, :], [('fo', ck) for ck in range(NCK)])
    else:
        store(P, 'o_x', xo_d, xT[:], [('x', ck) for ck in range(NCK)])
    return P.finish()


NCORES = 8


def fm(v):
    v = np.asarray(v)
    sh = v.shape
    v = v.reshape(sh[:-1] + (sh[-1] // 128, 128))
    return np.ascontiguousarray(np.moveaxis(v, -1, 0))


def rope_tables(r):
    f32 = np.float32
    pos = r * 1024 + np.arange(1024)
    row = (pos // 64).astype(f32)
    col = (pos % 64).astype(f32)
    out = {}
    for nm, ad in (('da', 32), ('g', 64)):
        n = ad // 2
        freqs = (f32(10000.0) ** (-(np.arange(n, dtype=f32) * f32(2.0) / f32(ad)))).astype(f32)
        ar = (row[:, None] * freqs).astype(f32)
        ac = (col[:, None] * freqs).astype(f32)
        cr, sr, cc, sc = np.cos(ar), np.sin(ar), np.cos(ac), np.sin(ac)
        ctab = np.concatenate([cr, cc], axis=1).astype(f32)
        stab = np.concatenate([-sr, sr, -sc, sc], axis=1).astype(f32)
        out['c' + nm] = np.ascontiguousarray(ctab.reshape(8, 128, 2 * n).transpose(1, 0, 2))
        out['s' + nm] = np.ascontiguousarray(stab.reshape(8, 128, 4 * n).transpose(1, 0, 2))
    return out


def inv_counts(r):
    out = np.zeros((4, T), np.float32)
    for g, w in enumerate((2, 4, 8, 16)):
        for (o0, n, L, p0) in ((0, 128, 256, r * 128), (128, 1024, 2048, r * 1024)):
            pos = p0 + np.arange(n)
            lo = np.clip(pos - w // 2, 0, L)
            hi = np.clip(pos - w // 2 + w, 0, L)
            out[g, o0:o0 + n] = 1.0 / (hi - lo).astype(np.float32)
    return np.ascontiguousarray(np.broadcast_to(out[None], (128, 4, T)))


def run(nc, in_maps):
    res = run_bass_kernel_spmd(nc, in_maps, core_ids=list(range(NCORES)))
    return res.results


def kernel(x, c, ctx, c_ctx, w_mod, b_mod, norm_mix, norm_ffn, w_in,
           da_lambda_q1, da_lambda_k1, da_lambda_q2, da_lambda_k2, da_subln,
           gqa_q_norm, gqa_k_norm, pool_w, pool_scale, gqa_out_norm, w_out,
           w_up, conv_w, conv_b, w_down, final_norm, _dbg=None):
    f32 = np.float32
    A = lambda v: np.asarray(v, dtype=f32)
    x, c, ctx, c_ctx = A(x), A(c), A(ctx), A(c_ctx)
    w_mod, b_mod, w_in, w_out, w_up, w_down = A(w_mod), A(b_mod), A(w_in), A(w_out), A(w_up), A(w_down)
    norm_mix, norm_ffn, final_norm = A(norm_mix), A(norm_ffn), A(final_norm)
    conv_w, conv_b, pool_w, pool_scale = A(conv_w), A(conv_b), A(pool_w), A(pool_scale)
    cores = [(b, r) for b in range(4) for r in range(2)]

    cc = np.concatenate([c, c_ctx[None]], axis=0)
    ccT = np.ascontiguousarray(cc.reshape(5, NCK, 128).transpose(2, 1, 0))
    ims = []
    for ci in range(NCORES):
        sl = slice(ci * 1536, (ci + 1) * 1536)
        ims.append(dict(ccT=ccT, wmod=np.ascontiguousarray(w_mod[:, :, sl]),
                        bmod=np.ascontiguousarray(b_mod[:, sl].reshape(2, 12, 128).transpose(2, 0, 1))))
    rs = run(build_mod(), ims)
    mod_full = np.zeros((2, 5, 6 * D), f32)
    for ci in range(NCORES):
        m = rs[ci]['modT'].reshape(128, 2, 12, 5)
        mod_full[:, :, ci * 1536:(ci + 1) * 1536] = m.transpose(1, 3, 2, 0).reshape(2, 5, 1536)
    mvs = []
    for (b, r) in cores:
        sel = mod_full[:, [b, 4], :].reshape(2, 2, 6, NCK, 128)
        mvs.append(np.ascontiguousarray(sel.transpose(4, 0, 2, 3, 1)).reshape(128, MV_N))
    gains = np.ascontiguousarray(np.stack([norm_mix, norm_ffn], axis=1).reshape(2, 2, NCK, 128).transpose(3, 0, 1, 2)).reshape(128, -1)

    xTs = []
    for (b, r) in cores:
        tok = np.concatenate([ctx[b, r * 128:(r + 1) * 128], x[b, r * 1024:(r + 1) * 1024]], axis=0)
        xTs.append(np.ascontiguousarray(tok.T.reshape(NCK, 128, T).transpose(1, 0, 2)))
    rts = [rope_tables(0), rope_tables(1)]
    ivc = [inv_counts(0), inv_counts(1)]
    bc = lambda v: np.ascontiguousarray(np.broadcast_to(np.asarray(v, f32)[None], (128,) + np.asarray(v).shape))

    for l in range(DEPTH):
        last = l == DEPTH - 1
        ims = []
        for ci, (b, r) in enumerate(cores):
            d = dict(xT=xTs[ci], w_in=w_in[l:l + 1], mv=mvs[ci], gains=gains, qn=bc(gqa_q_norm[l]), kn=bc(gqa_k_norm[l]))
            d.update(cda=rts[r]['cda'], sda=rts[r]['sda'], cg=rts[r]['cg'], sg=rts[r]['sg'])
            ims.append(d)
        rs = run(build_proj(l), ims)
        if _dbg is not None:
            _dbg[f'proj{l}'] = rs
            _dbg['mod_full'] = mod_full
            if _dbg.get('stop') == f'proj{l}':
                return None
        ims = []
        for ci, (b, r) in enumerate(cores):
            a0, a1 = rs[2 * b], rs[2 * b + 1]
            KTf = np.concatenate([a0['KT'], a1['KT']], axis=2)
            Vf = np.concatenate([a0['V'], a1['V']], axis=1)
            own, oth = rs[ci]['pexto'], rs[2 * b + 1 - r]['pexto']
            pext = np.array(own)
            LAT = PE_CTX
            if r == 0:
                pext[:, :, 0:8] = 0
                pext[:, :, LAT:LAT + 8] = 0
                pext[:, :, 136:144] = oth[:, :, 8:16]
                pext[:, :, LAT + 1032:LAT + 1040] = oth[:, :, LAT + 8:LAT + 16]
            else:
                pext[:, :, 136:144] = 0
                pext[:, :, LAT + 1032:LAT + 1040] = 0
                pext[:, :, 0:8] = oth[:, :, 128:136]
                pext[:, :, LAT:LAT + 8] = oth[:, :, LAT + 1024:LAT + 1032]
            lam4 = np.stack([da_lambda_q1[l], da_lambda_k1[l], da_lambda_q2[l], da_lambda_k2[l]], axis=0)
            ims.append(dict(xT=xTs[ci], QT=rs[ci]['QT'], KTf=np.ascontiguousarray(KTf), Vf=np.ascontiguousarray(Vf),
                            w_out=w_out[l:l + 1], mv=mvs[ci], gains=gains, pext=pext, invc=ivc[r],
                            pscale=fm(pool_scale[l]), lam4=bc(lam4), subln=fm(da_subln[l]), gon=fm(gqa_out_norm[l]),
                            wpool=pool_w[l]))
        rs = run(build_attn(l), ims)
        if _dbg is not None:
            _dbg[f'attn{l}'] = rs
            if _dbg.get('stop') == f'attn{l}':
                return None
        ims = []
        for ci, (b, r) in enumerate(cores):
            h2 = np.array(rs[ci]['h2e'])
            oth = rs[2 * b + 1 - r]['h2e']
            if r == 0:
                h2[:, :, 0] = 0
                h2[:, :, 130] = 0
                h2[:, :, 129] = oth[:, :, 1]
                h2[:, :, 1155] = oth[:, :, 131]
            else:
                h2[:, :, 129] = 0
                h2[:, :, 1155] = 0
                h2[:, :, 0] = oth[:, :, 128]
                h2[:, :, 130] = oth[:, :, 1154]
            cw = np.concatenate([conv_w[l], conv_b[l][None]], axis=0)
            ims.append(dict(xT=rs[ci]['xTo'], h2e=h2, w_up=w_up[l:l + 1], w_down=w_down[l:l + 1], mv=mvs[ci],
                            cw=np.ascontiguousarray(cw.reshape(4, 2 * NFC, 128).transpose(2, 1, 0)), fng=fm(final_norm)))
        rs = run(build_ffn(l, last), ims)
        if _dbg is not None:
            _dbg[f'ffn{l}'] = rs
            if _dbg.get('stop') == f'ffn{l}':
                return None
        xTs = [rs[ci]['xTo'] for ci in range(NCORES)]

    out = np.zeros((4, 2048, D), f32)
    for ci, (b, r) in enumerate(cores):
        o = xTs[ci][:, :, 128:]
        out[b, r * 1024:(r + 1) * 1024, :] = o.transpose(2, 1, 0).reshape(1024, D)
    return out
```
